# Optimizing a Trainium2 kernel written in Bass

```python
import math
import jax, jax.numpy as jnp
from jax import lax
import numpy as np

D_MODEL = 1024
BATCH = 16
SEQ = 4096
DEPTH = 1

N_META = 16
S5_WIDTH = D_MODEL // 2
S5_GROUP = 16
S5_GROUPS = S5_WIDTH // S5_GROUP
S5_STATE = 64
DT_MIN = 1e-3
DT_MAX = 1e-1
M_HEADS = 4
M_DK = D_MODEL // 8
M_DV = D_MODEL // 4
M_QK_WIDTH = M_HEADS * M_DK
M_V_WIDTH = M_HEADS * M_DV
M_CHUNK = 64
CONV_WIDTH = 4
D_FF = 4 * D_MODEL
ALPHA = (2.0 * DEPTH) ** 0.25
BETA = (8.0 * DEPTH) ** -0.25
LN_EPS = 1e-5
IN_SIZES = (S5_WIDTH, M_QK_WIDTH, M_QK_WIDTH, M_V_WIDTH, M_V_WIDTH, M_HEADS, M_HEADS, D_MODEL, D_MODEL)
IN_WIDTH = sum(IN_SIZES)
F_GATE_OFFSET = S5_WIDTH + 2 * M_QK_WIDTH + 2 * M_V_WIDTH + M_HEADS

kernel_name = "hybrid_s5_mlstm_gated_block"


def _layer_norm(x, g, b):
    xf = x.astype(jnp.float32)
    mu = jnp.mean(xf, axis=-1, keepdims=True)
    var = jnp.mean(jnp.square(xf - mu), axis=-1, keepdims=True)
    y = (xf - mu) * lax.rsqrt(var + LN_EPS)
    return (y * g.astype(jnp.float32) + b.astype(jnp.float32)).astype(x.dtype)


def _head_norm(h, g):
    hf = h.astype(jnp.float32)
    mu = jnp.mean(hf, axis=-1, keepdims=True)
    var = jnp.mean(jnp.square(hf - mu), axis=-1, keepdims=True)
    return (hf - mu) * lax.rsqrt(var + LN_EPS) * g.astype(jnp.float32).reshape(M_HEADS, M_DV)


def _split_columns(p):
    parts, start = [], 0
    for size in IN_SIZES:
        parts.append(p[..., start:start + size])
        start += size
    return parts


def _causal_depthwise_conv(x, w, b):
    c = x.shape[-1]
    y = lax.conv_general_dilated(
        x, w[:, None, :].astype(x.dtype), window_strides=(1,),
        padding=((CONV_WIDTH - 1, 0),), dimension_numbers=("NWC", "WIO", "NWC"),
        feature_group_count=c)
    return y + b


def _linear_recurrence_combine(left, right):
    a_l, b_l = left
    a_r, b_r = right
    return a_l * a_r, a_r * b_l + b_r


def _s5_mixer(u, lam_re, lam_im, log_dt, b_re, b_im, c_re, c_im, d_skip):
    f32 = jnp.float32
    bsz, length, _ = u.shape
    uf = u.astype(f32).reshape(bsz, length, S5_GROUPS, S5_GROUP)
    lam = lax.complex(lam_re.astype(f32), lam_im.astype(f32))
    dt = jnp.exp(log_dt.astype(f32))[:, None]
    lam_bar = jnp.exp(lam * dt)
    b_mat = lax.complex(b_re.astype(f32), b_im.astype(f32))
    b_bar = ((lam_bar - 1.0) / lam)[..., None] * b_mat
    bu = jnp.einsum("gph,blgh->blgp", b_bar, uf.astype(jnp.complex64))
    a = jnp.broadcast_to(lam_bar, (1, length, S5_GROUPS, S5_STATE))
    _, state = lax.associative_scan(_linear_recurrence_combine, (a, bu), axis=1)
    c_mat = lax.complex(c_re.astype(f32), c_im.astype(f32))
    y = jnp.real(jnp.einsum("ghp,blgp->blgh", c_mat, state))
    y = y + d_skip.astype(f32).reshape(S5_GROUPS, S5_GROUP) * uf
    return y.reshape(bsz, length, S5_WIDTH)


def _mlstm_mixer(q, k, v, i_pre, f_pre):
    f32 = jnp.float32
    bsz, length = q.shape[:2]
    n_pad = M_CHUNK - N_META
    n_chunks = (length + n_pad) // M_CHUNK

    def to_chunks(t, fill):
        t = t.astype(f32)
        t = jnp.pad(t, ((0, 0), (n_pad, 0)) + ((0, 0),) * (t.ndim - 2), constant_values=fill)
        t = t.reshape((bsz, n_chunks, M_CHUNK) + t.shape[2:])
        return jnp.moveaxis(t, (1, 3), (0, 2))

    qc = to_chunks(q, 0.0)
    kc = to_chunks(k * (M_DK ** -0.5), 0.0)
    vc = to_chunks(v, 0.0)
    log_i = to_chunks(i_pre, -jnp.inf)
    log_f = to_chunks(jax.nn.log_sigmoid(f_pre.astype(f32)), 0.0)
    causal = jnp.tril(jnp.ones((M_CHUNK, M_CHUNK), dtype=bool))

    def chunk_step(carry, inp):
        c_st, n_st, m_st = carry
        q_c, k_c, v_c, li_c, lf_c = inp
        b = jnp.cumsum(lf_c, axis=-1)
        d_mat = jnp.where(causal, b[..., :, None] - b[..., None, :] + li_c[..., None, :], -jnp.inf)
        m_inter = b + m_st[..., None]
        m_row = jnp.maximum(m_inter, jnp.max(d_mat, axis=-1))
        w_intra = jnp.exp(d_mat - m_row[..., None])
        w_inter = jnp.exp(m_inter - m_row)
        s = jnp.einsum("bhsd,bhjd->bhsj", q_c, k_c) * w_intra
        num = (jnp.einsum("bhsj,bhje->bhse", s, v_c)
               + w_inter[..., None] * jnp.einsum("bhsd,bhde->bhse", q_c, c_st))
        den = jnp.sum(s, axis=-1) + w_inter * jnp.einsum("bhsd,bhd->bhs", q_c, n_st)
        h_c = num / jnp.maximum(jnp.abs(den), jnp.exp(-m_row))[..., None]
        b_last = b[..., -1]
        g_log = b_last[..., None] - b + li_c
        m_new = jnp.maximum(b_last + m_st, jnp.max(g_log, axis=-1))
        w_k = jnp.exp(g_log - m_new[..., None])
        decay = jnp.exp(b_last + m_st - m_new)
        c_new = decay[..., None, None] * c_st + jnp.einsum("bhj,bhjd,bhje->bhde", w_k, k_c, v_c)
        n_new = decay[..., None] * n_st + jnp.einsum("bhj,bhjd->bhd", w_k, k_c)
        return (c_new, n_new, m_new), h_c

    init = (jnp.zeros((bsz, M_HEADS, M_DK, M_DV), f32),
            jnp.zeros((bsz, M_HEADS, M_DK), f32),
            jnp.zeros((bsz, M_HEADS), f32))
    _, h = lax.scan(chunk_step, init, (qc, kc, vc, log_i, log_f))
    h = jnp.moveaxis(h, (0, 2), (1, 3)).reshape(bsz, n_chunks * M_CHUNK, M_HEADS, M_DV)
    return h[:, n_pad:]


def setup_inputs(seed: int = 0) -> dict:
    key = jax.random.key(seed)
    ks = jax.random.split(key, 27)
    f32 = jnp.float32

    def nrm(k, shape, scale):
        return scale * jax.random.normal(k, shape, f32)

    b_in = nrm(ks[5], (DEPTH, IN_WIDTH), 0.02)
    b_in = b_in.at[:, F_GATE_OFFSET:F_GATE_OFFSET + M_HEADS].add(jnp.linspace(3.0, 6.0, M_HEADS))
    s5_lambda_re = -0.5 + nrm(ks[8], (DEPTH, S5_GROUPS, S5_STATE), 0.01)
    s5_lambda_im = jnp.pi * jnp.arange(S5_STATE, dtype=f32) + nrm(ks[9], (DEPTH, S5_GROUPS, S5_STATE), 0.01)
    s5_log_dt = jax.random.uniform(ks[10], (DEPTH, S5_GROUPS), f32, math.log(DT_MIN), math.log(DT_MAX))
    return {
        "x": nrm(ks[0], (BATCH, SEQ, D_MODEL), 1.0),
        "meta_tokens": nrm(ks[1], (N_META, D_MODEL), 1.0),
        "ln0_g": 1.0 + nrm(ks[2], (D_MODEL,), 0.05),
        "ln0_b": nrm(ks[3], (D_MODEL,), 0.02),
        "w_in": nrm(ks[4], (DEPTH, D_MODEL, IN_WIDTH), D_MODEL ** -0.5),
        "b_in": b_in,
        "qk_conv_w": nrm(ks[6], (DEPTH, CONV_WIDTH, 2 * M_QK_WIDTH), CONV_WIDTH ** -0.5),
        "qk_conv_b": nrm(ks[7], (DEPTH, 2 * M_QK_WIDTH), 0.02),
        "s5_lambda_re": s5_lambda_re,
        "s5_lambda_im": s5_lambda_im,
        "s5_log_dt": s5_log_dt,
        "s5_b_re": nrm(ks[11], (DEPTH, S5_GROUPS, S5_STATE, S5_GROUP), (2.0 * S5_GROUP) ** -0.5),
        "s5_b_im": nrm(ks[12], (DEPTH, S5_GROUPS, S5_STATE, S5_GROUP), (2.0 * S5_GROUP) ** -0.5),
        "s5_c_re": nrm(ks[13], (DEPTH, S5_GROUPS, S5_GROUP, S5_STATE), S5_STATE ** -0.5),
        "s5_c_im": nrm(ks[14], (DEPTH, S5_GROUPS, S5_GROUP, S5_STATE), S5_STATE ** -0.5),
        "s5_d": nrm(ks[15], (DEPTH, S5_WIDTH), 1.0),
        "s5_w_glu": nrm(ks[16], (DEPTH, S5_WIDTH, 2 * D_MODEL), S5_WIDTH ** -0.5),
        "m_norm_g": 1.0 + nrm(ks[17], (DEPTH, M_V_WIDTH), 0.05),
        "m_w_out": nrm(ks[18], (DEPTH, M_V_WIDTH, D_MODEL), M_V_WIDTH ** -0.5),
        "w_o": nrm(ks[19], (DEPTH, D_MODEL, D_MODEL), BETA * D_MODEL ** -0.5),
        "ln1_g": 1.0 + nrm(ks[20], (DEPTH, D_MODEL), 0.05),
        "ln1_b": nrm(ks[21], (DEPTH, D_MODEL), 0.02),
        "w_up": nrm(ks[22], (DEPTH, D_MODEL, D_FF), D_MODEL ** -0.5),
        "b_up": nrm(ks[23], (DEPTH, D_FF), 0.02),
        "w_down": nrm(ks[24], (DEPTH, D_FF, D_MODEL), BETA * D_FF ** -0.5),
        "ln2_g": 1.0 + nrm(ks[25], (DEPTH, D_MODEL), 0.05),
        "ln2_b": nrm(ks[26], (DEPTH, D_MODEL), 0.02),
    }


def reference(x, meta_tokens, ln0_g, ln0_b, w_in, b_in, qk_conv_w, qk_conv_b,
              s5_lambda_re, s5_lambda_im, s5_log_dt, s5_b_re, s5_b_im, s5_c_re, s5_c_im,
              s5_d, s5_w_glu, m_norm_g, m_w_out, w_o, ln1_g, ln1_b, w_up, b_up, w_down,
              ln2_g, ln2_b):
    bsz = x.shape[0]
    meta = jnp.broadcast_to(meta_tokens[None].astype(x.dtype), (bsz, N_META, D_MODEL))
    h = _layer_norm(jnp.concatenate([meta, x], axis=1), ln0_g, ln0_b)
    length = h.shape[1]
    for layer in range(DEPTH):
        p = h @ w_in[layer] + b_in[layer]
        u_s5, q, k, v, o_pre, i_pre, f_pre, g_s5, g_m = _split_columns(p)
        y_s5 = _s5_mixer(u_s5, s5_lambda_re[layer], s5_lambda_im[layer], s5_log_dt[layer],
                         s5_b_re[layer], s5_b_im[layer], s5_c_re[layer], s5_c_im[layer], s5_d[layer])
        z = jax.nn.gelu(y_s5).astype(h.dtype) @ s5_w_glu[layer]
        y_s5 = z[..., :D_MODEL] * jax.nn.sigmoid(z[..., D_MODEL:])
        qk = jax.nn.silu(_causal_depthwise_conv(jnp.concatenate([q, k], axis=-1),
                                                qk_conv_w[layer], qk_conv_b[layer]))
        q_h = qk[..., :M_QK_WIDTH].reshape(bsz, length, M_HEADS, M_DK)
        k_h = qk[..., M_QK_WIDTH:].reshape(bsz, length, M_HEADS, M_DK)
        v_h = v.reshape(bsz, length, M_HEADS, M_DV)
        hm = _mlstm_mixer(q_h, k_h, v_h, i_pre, f_pre)
        hm = _head_norm(hm, m_norm_g[layer]).reshape(bsz, length, M_V_WIDTH).astype(h.dtype)
        y_m = (jax.nn.sigmoid(o_pre) * hm) @ m_w_out[layer]
        mix = jax.nn.sigmoid(g_s5) * y_s5 + jax.nn.sigmoid(g_m) * y_m
        h = _layer_norm(ALPHA * h + mix @ w_o[layer], ln1_g[layer], ln1_b[layer])
        ff = jnp.square(jax.nn.relu(h @ w_up[layer] + b_up[layer])) @ w_down[layer]
        h = _layer_norm(ALPHA * h + ff, ln2_g[layer], ln2_b[layer])
    return h[:, N_META:]
```

```python
import math
from contextlib import ExitStack

import numpy as np
import concourse.bass as bass
import concourse.mybir as mybir
from concourse.bass_utils import run_bass_kernel_spmd

F32 = mybir.dt.float32
BF16 = mybir.dt.bfloat16
ALU = mybir.AluOpType
AF = mybir.ActivationFunctionType
AX = mybir.AxisListType

D = 1024
SEQ = 4096
NMETA = 16
NPAD = 48
T = 320
NSTEP = 13
TC = 8
NCH = T // TC
MCH = 64
NMC = T // MCH
DK = 128
DV = 256
NH = 4
DFF = 4096
ALPHA = 2.0 ** 0.25
LN_EPS = 1e-5
O_U, O_Q, O_K, O_V, O_O, O_I, O_F, O_GS, O_GM = 0, 512, 1024, 1536, 2560, 3584, 3588, 3592, 4616
IN_W = 5640
TOK_TILES = [(0, 128), (128, 128), (256, 64)]


class Op:
    __slots__ = ("issuer", "fn", "deps", "sem", "inc", "val", "needed", "is_dma")


class Sched:
    def __init__(self):
        self.streams = {e: [] for e in ("sp", "pe", "act", "dve", "pool")}
        self.all = []
        self.last_w = {}
        self.readers = {}
        self.default_owner = None
        self.group_sems = set()

    def add(self, issuer, fn, reads=(), writes=(), dma=None):
        op = Op()
        op.issuer = issuer
        op.fn = fn
        op.needed = False
        op.is_dma = dma is not None
        op.sem = ("dma", dma) if dma else ("eng", issuer)
        op.inc = 16 if dma else 1
        deps = {}
        for k in reads:
            lw = self.last_w.get(k, self.default_owner)
            if lw is not None:
                deps[id(lw)] = lw
        for k in writes:
            lw = self.last_w.get(k, self.default_owner)
            if lw is not None:
                deps[id(lw)] = lw
            for r in self.readers.get(k, {}).values():
                deps[id(r)] = r
        dl = []
        for d in deps.values():
            if d is op:
                continue
            if issuer == "pe" and d.sem == ("eng", "pe"):
                continue
            if op.is_dma and d.is_dma and d.sem == op.sem and op.sem[1] in self.group_sems:
                continue
            dl.append(d)
        op.deps = dl
        for k in writes:
            self.last_w[k] = op
            self.readers[k] = {}
        for k in reads:
            self.readers.setdefault(k, {})[op.sem] = op
        self.streams[issuer].append(op)
        self.all.append(op)
        return op

    def finalize(self):
        for op in self.all:
            for d in op.deps:
                d.needed = True
        cnt = {}
        for op in self.all:
            if op.needed or op.is_dma:
                cnt[op.sem] = cnt.get(op.sem, 0) + op.inc
                op.val = cnt[op.sem]
            else:
                op.val = None
        for op in self.all:
            if op.is_dma and op.sem[1] in self.group_sems:
                op.val = cnt[op.sem]
        return cnt

    def check_no_deadlock(self):
        pos = {k: 0 for k in self.streams}
        cnt = {}
        progress = True
        while progress:
            progress = False
            for k, st in self.streams.items():
                while pos[k] < len(st):
                    op = st[pos[k]]
                    if all(cnt.get(d.sem, 0) >= d.val for d in op.deps):
                        if op.needed or op.is_dma:
                            cnt[op.sem] = cnt.get(op.sem, 0) + op.inc
                        pos[k] += 1
                        progress = True
                    else:
                        break
        stuck = {k: pos[k] for k in self.streams if pos[k] < len(self.streams[k])}
        assert not stuck, f"deadlock in schedule: {stuck}"

    def emit(self, issuer, e, sems):
        seen = {}
        for op in self.streams[issuer]:
            for d in op.deps:
                if seen.get(d.sem, 0) < d.val:
                    e.wait_ge(sems[d.sem], d.val)
                    seen[d.sem] = d.val
            ins = op.fn(e)
            if op.needed or op.is_dma:
                ins.then_inc(sems[op.sem], op.inc)


class _Stop(Exception):
    pass


def build_program(dbg=None, dbg_at=None, dbg_taps_fn=None, nsteps_limit=None, phase_limit=None):
    nc = bass.Bass("TRN2", target_bir_lowering=False)
    S = Sched()
    es = ExitStack()

    def din(name, shape, dt=F32):
        return nc.dram_tensor(name, list(shape), dt, kind="ExternalInput").ap()

    x2 = din("x2", [2, SEQ, D])
    meta = din("meta_tokens", [NMETA, D])
    ln0_g = din("ln0_g", [D]); ln0_b = din("ln0_b", [D])
    w_in = din("w_in", [D, IN_W]); b_in = din("b_in", [IN_W])
    conv_w = din("qk_conv_w", [4, 1024]); conv_b = din("qk_conv_b", [1024])
    lam_re = din("s5_lambda_re", [32, 64]); lam_im = din("s5_lambda_im", [32, 64])
    log_dt = din("s5_log_dt", [32])
    sb_re = din("s5_b_re", [32, 64, 16]); sb_im = din("s5_b_im", [32, 64, 16])
    sc_re = din("s5_c_re", [32, 16, 64]); sc_im = din("s5_c_im", [32, 16, 64])
    s5_d = din("s5_d", [512])
    w_glu = din("s5_w_glu", [512, 2048])
    m_norm_g = din("m_norm_g", [1024])
    m_w_out = din("m_w_out", [1024, 1024])
    w_o = din("w_o", [1024, 1024])
    ln1_g = din("ln1_g", [D]); ln1_b = din("ln1_b", [D])
    w_up = din("w_up", [D, DFF]); b_up = din("b_up", [DFF])
    w_down = din("w_down", [DFF, D])
    ln2_g = din("ln2_g", [D]); ln2_b = din("ln2_b", [D])
    c_ident = din("c_ident", [128, 128]); c_bd16 = din("c_bd16", [128, 128])
    c_caus = din("c_caus", [64, 64]); c_par = din("c_par", [128, 4])
    c_seg = din("c_seg", [4, T]); c_sel4 = din("c_sel4", [4, 512])
    y_out = nc.dram_tensor("y", [2, SEQ, D], F32, kind="ExternalOutput").ap()
    dbg_out = {}
    if dbg:
        for name, shape in dbg.items():
            dbg_out[name] = nc.dram_tensor("dbg_" + name, list(shape), F32, kind="ExternalOutput").ap()

    def sb(name, shape, dt=F32):
        return es.enter_context(nc.sbuf_tensor(name, list(shape), dt))

    def ps(name, shape, dt=F32):
        return es.enter_context(nc.psum_tensor(name, list(shape), dt))

    def act(out, in_, func, r, w, bias=None, scale=None):
        kw = {}
        if bias is not None:
            kw["bias"] = bias
        if scale is not None:
            kw["scale"] = scale
        return S.add("act", lambda e: e.activation(out, in_, func, **kw), r, w)

    def tt(eng, out, in0, in1, op, r, w):
        return S.add(eng, lambda e: e.tensor_tensor(out, in0, in1, op), r, w)

    def ts(eng, out, in0, s1, s2, op0, op1, r, w):
        if s2 is None:
            return S.add(eng, lambda e: e.tensor_scalar(out, in0, s1, None, op0), r, w)
        return S.add(eng, lambda e: e.tensor_scalar(out, in0, s1, s2, op0, op1), r, w)

    def stt(out, in0, scalar, in1, op0, op1, r, w):
        return S.add("dve", lambda e: e.scalar_tensor_tensor(out, in0, scalar, in1, op0, op1), r, w)

    def cp(eng, out, in_, r, w):
        if eng == "act":
            return S.add("act", lambda e: e.copy(out, in_), r, w)
        return S.add(eng, lambda e: e.tensor_copy(out, in_), r, w)

    def mset(eng, out, val, w):
        return S.add(eng, lambda e: e.memset(out, val), (), w)

    def mm(out, lhsT, rhs, start, stop, r, w, **kw):
        return S.add("pe", lambda e: e.matmul(out, lhsT, rhs, start=start, stop=stop, **kw), r, w)

    def tr(out, in_, ident, r, w):
        return S.add("pe", lambda e: e.transpose(out, in_, ident), r, w)

    def dma(issuer, out, in_, r, w, sem, **kw):
        if sem == "const":
            w = tuple(w) + ("constq",)
        return S.add(issuer, lambda e: e.dma_start(out=out, in_=in_, **kw), r, w, dma=sem)

    def dap(ap, offset, dims):
        return bass.AP(ap.tensor, offset, dims)

    ident = sb("ident", [128, 128]); identb = sb("identb", [128, 128], BF16)
    bd16 = sb("bd16", [128, 128]); caus = sb("caus", [64, 64]); causb = sb("causb", [64, 64], BF16)
    par = sb("par", [128, 4]); seg = sb("seg", [4, T]); sel4 = sb("sel4", [4, 512])
    ones64 = sb("ones64", [64, 1], BF16)
    bias_fm = sb("bias_fm", [128, 44])
    bias_g = sb("bias_g", [4, 2])
    bup = sb("bup", [128, 32])
    cw = sb("cw", [128, 8, 4]); cb = sb("cb", [128, 8])
    d5 = sb("d5", [128, 4]); mng = sb("mng", [128, 8])
    wg32 = sb("wg32", [128, 8, 8]); wg = sb("wg", [128, 8, 8], BF16)
    CK = "const"
    cl = []
    cl.append((ident[:], c_ident)); cl.append((bd16[:], c_bd16)); cl.append((caus[:], c_caus))
    cl.append((par[:], c_par)); cl.append((seg[:], c_seg)); cl.append((sel4[:], c_sel4))
    cl.append((bias_fm[:, 0:28], b_in[0:3584].rearrange("(c p) -> p c", p=128)))
    cl.append((bias_fm[:, 28:44], b_in[O_GS:IN_W].rearrange("(c p) -> p c", p=128)))
    cl.append((bias_g[:, 0:2], b_in[O_I:O_I + 8].rearrange("(c p) -> p c", p=4)))
    cl.append((bup[:], b_up.rearrange("(c p) -> p c", p=128)))
    for w_ in range(4):
        cl.append((cw[:, :, w_], conv_w[w_, :].rearrange("(t p) -> p t", p=128)))
    cl.append((cb[:], conv_b.rearrange("(t p) -> p t", p=128)))
    cl.append((d5[:], s5_d.rearrange("(c p) -> p c", p=128)))
    cl.append((mng[:], m_norm_g.rearrange("(c p) -> p c", p=128)))
    cl.append((wg32[:], w_in[:, O_I:O_I + 8].rearrange("(k p) c -> p k c", p=128)))
    for o_, i_ in cl:
        dma("sp", o_, i_, (), (CK,), "const", allow_slow_non_contiguous=True)
    cp("dve", identb[:], ident[:], (CK,), ("identb",))
    cp("dve", causb[:], caus[:], (CK,), ("causb",))
    cp("dve", wg[:], wg32[:], (CK,), ("wg",))
    mset("dve", ones64[:], 1.0, ("ones64",))
    epsc = sb("epsc", [128, 1]); lnkc = sb("lnkc", [128, 1]); onec = sb("onec", [128, 1])
    mset("dve", epsc[:], LN_EPS, ("epsc",))
    mset("dve", lnkc[:], -0.5 * math.log(DK), ("lnkc",))
    mset("dve", onec[:], 1.0, ("onec",))
    bvrow32 = sb("bvrow32", [1, 1024]); bvrow = sb("bvrow", [1, 1024], BF16); ones1 = sb("ones1", [1, 64], BF16)
    dma("sp", bvrow32[:], b_in[O_V:O_V + 1024].rearrange("(o c) -> o c", o=1), (), ("bvrow32",), "const")
    cp("dve", bvrow[:], bvrow32[:], ("bvrow32",), ("bvrow",))
    mset("dve", ones1[:], 1.0, ("ones1",))

    PB = [ps(f"pb{i}", [128, 512]) for i in range(8)]
    PBK = [f"pb{i}" for i in range(8)]

    s5W = sb("s5W", [128, 4, TC, 2, 128], BF16)
    s5K = sb("s5K", [128, 4, TC, 128], BF16)
    s5C = sb("s5C", [128, TC, 2, 16, 32], BF16)
    Epos = sb("Epos", [128, 2, 16, NCH])
    D0 = sb("D0", [128, 16, NCH])
    Gc = sb("Gc", [128, 2, 16])
    with ExitStack() as tmp:
        def tb(name, shape, dt=F32):
            return tmp.enter_context(nc.sbuf_tensor(name, list(shape), dt))
        lre = tb("lre", [128, 16]); lim = tb("lim", [128, 16]); ldt = tb("ldt", [128, 16])
        Bin = tb("Bin", [128, 2, 16, 16]); Cin = tb("Cin", [128, 2, 4, 64])
        for t_ in range(2):
            dma("sp", lre[64 * t_:64 * t_ + 64, :], dap(lam_re, 64 * t_, [[1, 64], [128, 16]]), (), ("s5in",), "const",
                allow_slow_non_contiguous=True)
            dma("sp", lim[64 * t_:64 * t_ + 64, :], dap(lam_im, 64 * t_, [[1, 64], [128, 16]]), (), ("s5in",), "const",
                allow_slow_non_contiguous=True)
            dma("sp", ldt[64 * t_:64 * t_ + 64, :], dap(log_dt, t_, [[0, 64], [2, 16]]), (), ("s5in",), "const",
                allow_slow_non_contiguous=True)
            dma("sp", Bin[64 * t_:64 * t_ + 64, 0, :, :], dap(sb_re, 1024 * t_, [[16, 64], [2048, 16], [1, 16]]),
                (), ("s5in",), "const")
            dma("sp", Bin[64 * t_:64 * t_ + 64, 1, :, :], dap(sb_im, 1024 * t_, [[16, 64], [2048, 16], [1, 16]]),
                (), ("s5in",), "const")
        dma("sp", Cin[:, 0, :, :], sc_re.rearrange("(c g) h p -> (g h) c p", c=4), (), ("s5in",), "const")
        dma("sp", Cin[:, 1, :, :], sc_im.rearrange("(c g) h p -> (g h) c p", c=4), (), ("s5in",), "const")

        dtt = tb("dtt", [128, 16]); xr = tb("xr", [128, 16]); th = tb("th", [128, 16]); thc = tb("thc", [128, 16])
        tmp1 = tb("tmp1", [128, 16]); tmp2 = tb("tmp2", [128, 16]); tmp3 = tb("tmp3", [128, 16])
        mag = tb("mag", [128, 16]); sn = tb("sn", [128, 16]); cs = tb("cs", [128, 16])
        pw = tb("pw", [128, TC + 1, 2, 16])
        coef = tb("coef", [128, 2, 16])
        K_ = ("s5t",)
        act(dtt[:], ldt[:], AF.Exp, ("s5in",), K_)
        tt("pool", xr[:], lre[:], dtt[:], ALU.mult, K_ + ("s5in",), K_)
        tt("pool", th[:], lim[:], dtt[:], ALU.mult, K_ + ("s5in",), K_)
        ts("pool", thc[:], th[:], math.pi / 2, None, ALU.add, None, K_, K_)
        for ang in (th, thc):
            cp("pool", tmp2[:], ang[:], K_, K_)
            for m_ in range(1, 8):
                ts("dve", tmp1[:], ang[:], (2 * m_ - 1) * math.pi, -2 * math.pi, ALU.is_ge, ALU.mult, K_, K_)
                tt("pool", tmp2[:], tmp2[:], tmp1[:], ALU.add, K_, K_)
            cp("pool", ang[:], tmp2[:], K_, K_)
        act(mag[:], xr[:], AF.Exp, K_, K_)
        act(sn[:], th[:], AF.Sin, K_, K_)
        act(cs[:], thc[:], AF.Sin, K_, K_)
        ar = pw[:, 1, 0, :]; ai = pw[:, 1, 1, :]
        mset("pool", pw[:, 0, 0, :], 1.0, K_)
        mset("pool", pw[:, 0, 1, :], 0.0, K_)
        tt("pool", ar, mag[:], cs[:], ALU.mult, K_, K_)
        tt("pool", ai, mag[:], sn[:], ALU.mult, K_, K_)

        def cmul(eng, o_r, o_i, a_r, a_i, b_r, b_i, t1, t2, conj_b=False):
            tt(eng, t1, a_r, b_r, ALU.mult, K_, K_)
            tt(eng, t2, a_i, b_i, ALU.mult, K_, K_)
            tt(eng, t1, t1, t2, ALU.add if conj_b else ALU.subtract, K_, K_)
            tt(eng, t2, a_i, b_r, ALU.mult, K_, K_)
            tt(eng, o_r, a_r, b_i, ALU.mult, K_, K_)
            tt(eng, o_i, t2, o_r, ALU.subtract if conj_b else ALU.add, K_, K_)
            cp(eng, o_r, t1, K_, K_)

        for tau in range(2, TC + 1):
            cmul("pool", pw[:, tau, 0, :], pw[:, tau, 1, :], pw[:, tau - 1, 0, :], pw[:, tau - 1, 1, :], ar, ai,
                 tmp1[:], tmp2[:])
        nr = tb("nr", [128, 16]); den = tb("den", [128, 16])
        ts("pool", nr[:], ar, -1.0, None, ALU.add, None, K_, K_)
        tt("pool", den[:], lre[:], lre[:], ALU.mult, K_, K_)
        tt("pool", tmp1[:], lim[:], lim[:], ALU.mult, K_, K_)
        tt("pool", den[:], den[:], tmp1[:], ALU.add, K_, K_)
        S.add("dve", lambda e: e.reciprocal(den[:], den[:]), K_, K_)
        cmul("pool", coef[:, 0, :], coef[:, 1, :], nr[:], ai, lre[:], lim[:], tmp1[:], tmp2[:], conj_b=True)
        tt("pool", coef[:, 0, :], coef[:, 0, :], den[:], ALU.mult, K_, K_)
        tt("pool", coef[:, 1, :], coef[:, 1, :], den[:], ALU.mult, K_, K_)
        rA = tb("rA", [128, 16]); rinv = tb("rinv", [128, 16]); eph = tb("eph", [128, 2, 16])
        act(rA[:], xr[:], AF.Exp, K_, K_, scale=float(TC))
        act(rinv[:], xr[:], AF.Exp, K_, K_, scale=-float(TC))
        tt("pool", eph[:, 0, :], pw[:, TC, 0, :], rinv[:], ALU.mult, K_, K_)
        tt("pool", eph[:, 1, :], pw[:, TC, 1, :], rinv[:], ALU.mult, K_, K_)
        pk = tb("pk", [128, 2, 16]); pk2 = tb("pk2", [128, 2, 16])
        bt1 = tb("bt1", [128, 16, 16]); bt2 = tb("bt2", [128, 16, 16])
        mset("pool", Epos[:, 0, :, 0:1], 1.0, K_)
        mset("pool", Epos[:, 1, :, 0:1], 0.0, K_)
        cp("pool", pk[:], eph[:], K_, K_)
        n_ = 1
        while n_ < NCH:
            m_ = min(n_, NCH - n_)
            pr = pk[:, 0, :].unsqueeze(2).broadcast_to([128, 16, m_])
            pi_ = pk[:, 1, :].unsqueeze(2).broadcast_to([128, 16, m_])
            cmul("pool", Epos[:, 0, :, n_:n_ + m_], Epos[:, 1, :, n_:n_ + m_], Epos[:, 0, :, 0:m_], Epos[:, 1, :, 0:m_],
                 pr, pi_, bt1[:, :, 0:m_], bt2[:, :, 0:m_])
            cmul("pool", pk2[:, 0, :], pk2[:, 1, :], pk[:, 0, :], pk[:, 1, :], pk[:, 0, :], pk[:, 1, :], tmp1[:], tmp2[:])
            cp("pool", pk[:], pk2[:], K_, K_)
            n_ *= 2
        cmul("pool", Gc[:, 0, :], Gc[:, 1, :], Epos[:, 0, :, NCH - 1], Epos[:, 1, :, NCH - 1], eph[:, 0, :], eph[:, 1, :],
             tmp1[:], tmp2[:])
        tt("pool", Gc[:, 0, :], Gc[:, 0, :], rA[:], ALU.mult, K_, K_)
        tt("pool", Gc[:, 1, :], Gc[:, 1, :], rA[:], ALU.mult, K_, ("s5c",) + K_)
        cp("pool", D0[:], rA[:].unsqueeze(2).broadcast_to([128, 16, NCH]), K_, K_)
        mset("pool", D0[:, :, 0:1], 0.0, ("s5c",) + K_)
        Cz = tb("Cz", [128, 2, 4, 128]); Cd = tb("Cd", [128, 2, 4, 128]); Cdn = tb("Cdn", [128, 4, 128])
        for ri in range(2):
            for t_ in range(2):
                ts("pool", Cz[:, ri, :, 64 * t_:64 * t_ + 64], Cin[:, ri, :, :], par[:, 2 + t_:3 + t_], None, ALU.mult, None,
                   K_ + ("s5in", CK), K_)
            for c in range(4):
                tr(PB[4][:, 128 * c:128 * c + 128], Cz[:, ri, c, :], ident[:], K_ + (CK,), (PBK[4],))
            cp("dve", Cd[:, ri, :, :], PB[4][:].rearrange("p (c x) -> p c x", c=4), (PBK[4],), K_)
        ts("pool", Cdn[:], Cd[:, 1, :, :], -1.0, None, ALU.mult, None, K_, K_)
        Bc = tb("Bc", [128, 2, 16, 16]); Bd = tb("Bd", [128, 2, 16, 32]); Bd2 = tb("Bd2", [128, 2, 16, 32])
        c_r = coef[:, 0, :].unsqueeze(2).broadcast_to([128, 16, 16])
        c_i = coef[:, 1, :].unsqueeze(2).broadcast_to([128, 16, 16])
        cmul("pool", Bc[:, 0, :, :], Bc[:, 1, :, :], Bin[:, 0, :, :], Bin[:, 1, :, :], c_r, c_i, bt1[:], bt2[:])
        for ri in range(2):
            for t_ in range(2):
                ts("pool", Bd[:, ri, :, 16 * t_:16 * t_ + 16], Bc[:, ri, :, :], par[:, t_:t_ + 1], None, ALU.mult, None,
                   K_ + (CK,), K_)
        Ktmp = tb("Ktmp", [128, 128])
        bw1 = tb("bw1", [128, 16, 32]); bw2 = tb("bw2", [128, 16, 32])
        a_r32 = ar.unsqueeze(2).broadcast_to([128, 16, 32]); a_i32 = ai.unsqueeze(2).broadcast_to([128, 16, 32])
        for tau in range(TC):
            for ri in range(2):
                for c in range(4):
                    tr(PB[4 + ri][:, 128 * c:128 * c + 128], Bd[:, ri, 4 * c:4 * c + 4, :].rearrange("p q x -> p (q x)"),
                       ident[:], K_ + (CK,), (PBK[4 + ri],))
                cp("dve", s5W[:, :, tau, ri, :], PB[4 + ri][:].rearrange("p (c x) -> p c x", c=4), (PBK[4 + ri],),
                   ("s5c",))
            for c in range(4):
                pk_ = PB[6 + (c % 2)]
                mm(pk_[:, 0:128], Bd[:, 0, 4 * c:4 * c + 4, :].rearrange("p q x -> p (q x)"), Cd[:, 0, c, :], True, False,
                   K_, (PBK[6 + (c % 2)],))
                mm(pk_[:, 0:128], Bd[:, 1, 4 * c:4 * c + 4, :].rearrange("p q x -> p (q x)"), Cdn[:, c, :], False, True,
                   K_, (PBK[6 + (c % 2)],))
                if tau == 0:
                    tt("dve", Ktmp[:], pk_[:, 0:128], bd16[:], ALU.mult, (PBK[6 + (c % 2)], CK), K_)
                    stt(s5K[:, c, tau, :], ident[:], d5[:, c:c + 1], Ktmp[:], ALU.mult, ALU.add, K_ + (CK,), ("s5c",))
                else:
                    tt("dve", s5K[:, c, tau, :], pk_[:, 0:128], bd16[:], ALU.mult, (PBK[6 + (c % 2)], CK), ("s5c",))
            p_r = pw[:, tau + 1, 0, :].unsqueeze(2).broadcast_to([128, 16, 32])
            p_i = pw[:, tau + 1, 1, :].unsqueeze(2).broadcast_to([128, 16, 32])
            Cdr = Cd[:, 0, :, :].rearrange("p c (q x) -> p (c q) x", q=4)
            Cdi = Cd[:, 1, :, :].rearrange("p c (q x) -> p (c q) x", q=4)
            tt("pool", bw1[:], Cdr, p_r, ALU.mult, K_, K_)
            tt("pool", bw2[:], Cdi, p_i, ALU.mult, K_, K_)
            tt("pool", s5C[:, tau, 0, :, :], bw1[:], bw2[:], ALU.subtract, K_, ("s5c",))
            tt("pool", bw1[:], Cdr, p_i, ALU.mult, K_, K_)
            tt("pool", bw2[:], Cdi, p_r, ALU.mult, K_, K_)
            tt("pool", bw1[:], bw1[:], bw2[:], ALU.add, K_, K_)
            ts("pool", s5C[:, tau, 1, :, :], bw1[:], -1.0, None, ALU.mult, None, K_, ("s5c",))
            if tau < TC - 1:
                cmul("pool", Bd2[:, 0, :, :], Bd2[:, 1, :, :], Bd[:, 0, :, :], Bd[:, 1, :, :], a_r32, a_i32, bw1[:], bw2[:])
                cp("pool", Bd[:], Bd2[:], K_, K_)
    fdummy = sb("fdummy", [128, 1])
    fence = mset("pool", fdummy[:], 0.0, ("s5t", "s5in", "fdummy"))
    S.default_owner = fence
    PBb4 = PB[4][:].bitcast(BF16)
    PBb6 = PB[6][:].bitcast(BF16)

    WSLOTS = 4
    PCOLS = 256
    wslot = [sb(f"wslot{i}", [128, 2048], BF16) for i in range(WSLOTS)]
    gb = [sb(f"gb{i}", [128, 1024]) for i in range(2)]
    h_tok = sb("h_tok", [128, 3, 1024])
    hT = sb("hT", [128, 8, T], BF16)
    lnst = sb("lnst", [128, 12]); lnmv = sb("lnmv", [128, 2]); lnr = sb("lnr", [128, 2])
    uT = sb("uT", [128, 4, T], BF16)
    V5 = sb("V5", [128, 2, 16, NCH]); Vt = sb("Vt", [128, 2, 16, NCH])
    Y5 = V5
    s5a = sb("s5a", [128, 16, NCH]); s5b = sb("s5b", [128, 16, NCH])
    Xh = [sb(f"Xh{q}", [128, 2, 16, NCH + 1], BF16) for q in range(2)]
    Ylast = [sb(f"Ylast{q}", [128, 2, 16]) for q in range(2)]
    s5i = sb("s5i", [128, 4, 16])
    yS = sb("yS", [128, T]); g1 = sb("g1", [128, T]); g2 = sb("g2", [128, T])
    gT = sb("gT", [128, 4, T], BF16)
    sgzt = [sb(f"sgzt{i}", [128, T], BF16) for i in range(2)]
    sgs = sb("sgs", [128, 8, T], BF16)
    raw = [sb(f"raw{i}", [128, T + 3]) for i in range(2)]
    cacc = [sb(f"cacc{i}", [128, T]) for i in range(2)]
    halo = [sb(f"halo{q}", [128, 8, 3]) for q in range(2)]
    qkT = sb("qkT", [128, 8, T], BF16)
    mixT = qkT
    v_tok = sb("v_tok", [64, NMC, 1024], BF16)
    sgo = sb("sgo", [128, 8, T], BF16)
    gatedT = sb("gatedT", [128, 8, T], BF16)
    St = [sb(f"St{q}", [128, NH, DV]) for q in range(2)]
    nS = [sb(f"nS{q}", [128, NH]) for q in range(2)]
    Sbf = sb("Sbf", [128, NH, DV], BF16); nbf = sb("nbf", [128, NH], BF16)
    mcar = [sb(f"mcar{q}", [4, 1]) for q in range(2)]
    gli = sb("gli", [4, T]); glf = sb("glf", [4, T]); gb_ = sb("gbcum", [4, T]); gz = sb("gz", [4, T])
    gw = sb("gw", [4, T]); grho = sb("grho", [4, T])
    gM = sb("gM", [4, NMC]); gbl = sb("gbl", [4, NMC]); gma = sb("gma", [4, NMC]); gms = sb("gms", [4, NMC])
    gmu = sb("gmu", [4, NMC]); gdl = sb("gdl", [4, NMC])
    og = sb("og", [64, NMC, 2, 4])
    dlb = sb("dlb", [128, NH, NMC])
    sTa = sb("sTa", [64, NH, 64]); sTm = sb("sTm", [64, NH, 64], BF16)
    ktl = sb("ktl", [64, NH, 128], BF16)
    hden = sb("hden", [64, 4]); hr = sb("hr", [64, 4]); hst = sb("hst", [64, 4, 6]); hmv = sb("hmv", [64, 4, 2])
    hsc = sb("hsc", [64, 4]); hbi = sb("hbi", [64, 4]); hv1 = sb("hv1", [64, 4])
    hn = sb("hn", [64, 1024], BF16)
    actT = [gatedT, sgo]
    actK = ["gatedT", "sgo"]
    rl = [sb(f"rl{i}", [128, T]) for i in range(2)]
    gtmp = sb("gtmp", [128, T], BF16)
    print("sbuf bytes remaining/partition:", nc.sbuf_bytes_remaining)

    for q in range(2):
        mset("pool", St[q][:], 0.0, (f"St{q}",))
        mset("pool", nS[q][:], 0.0, (f"nS{q}",))
        mset("pool", mcar[q][:], 0.0, (f"mcar{q}",))
        mset("pool", Xh[q][:], 0.0, (f"Xh{q}",))
        mset("pool", Ylast[q][:], 0.0, (f"Ylast{q}",))
        mset("pool", halo[q][:], 0.0, (f"halo{q}",))

    pieces = {}

    def def_piece(name, src_ap, nk, ncols):
        scr = nc.dram_tensor("scr_" + name, [128, nk * ncols], BF16).ap()
        pieces[name] = dict(src=src_ap, nk=nk, nc=ncols, scr=scr)

    def wcols(w, r0, nrows, c0, ncols):
        return w[r0:r0 + nrows, c0:c0 + ncols].rearrange("(k p) c -> p k c", p=128)

    for nm, off, tot in (("u", O_U, 512), ("gs", O_GS, 1024), ("q", O_Q, 512), ("k", O_K, 512), ("v", O_V, 1024),
                         ("o", O_O, 1024), ("gm", O_GM, 1024)):
        for i in range(tot // PCOLS):
            def_piece(f"in_{nm}{i}", wcols(w_in, 0, 1024, off + PCOLS * i, PCOLS), 8, PCOLS)
    for i in range(2):
        def_piece(f"glu2_{i}", wcols(w_glu, 0, 512, 1024 + 512 * i, 512), 4, 512)
        def_piece(f"glu1_{i}", wcols(w_glu, 0, 512, 512 * i, 512), 4, 512)
    for i in range(4):
        def_piece(f"mwo{i}", wcols(m_w_out, 0, 1024, PCOLS * i, PCOLS), 8, PCOLS)
        def_piece(f"wo{i}", wcols(w_o, 0, 1024, PCOLS * i, PCOLS), 8, PCOLS)
    for qf in range(4):
        for i in range(4):
            def_piece(f"up{qf}_{i}", wcols(w_up, 0, 1024, 1024 * qf + PCOLS * i, PCOLS), 8, PCOLS)
            def_piece(f"dn{qf}_{i}", wcols(w_down, 1024 * qf, 1024, PCOLS * i, PCOLS), 8, PCOLS)

    wstate = dict(next=0, first=True)

    def load_piece(name):
        pc = pieces[name]
        i = wstate["next"] % WSLOTS
        wstate["next"] += 1
        key = f"ws{i}"
        n_ = pc["nk"] * pc["nc"]
        view = wslot[i][:, 0:n_].rearrange("p (k c) -> p k c", k=pc["nk"])
        if wstate["first"]:
            dma("pool", view, pc["src"], (), (key,), f"wl{i}")
            dma("sp", pc["scr"], wslot[i][:, 0:n_], (key,), ("scr_" + name,), f"wst{i}")
        else:
            dma("sp", wslot[i][:, 0:n_], pc["scr"], ("scr_" + name,), (key,), f"wl{i}")
        return view, key

    gbstate = dict(next=0)

    def load_gb(vec):
        i = gbstate["next"] % 2
        gbstate["next"] += 1
        dma("sp", gb[i][:], vec.partition_broadcast(128), (), (f"gb{i}",), f"gbl{i}")
        return gb[i], f"gb{i}"

    def layer_norm(src, rows, key, gt, gk, bt, bk):
        r = slice(0, rows)
        S.add("dve", lambda e: e.bn_stats(lnst[r, 0:6], src[r, 0:512]), (key,), ("lnst",))
        S.add("dve", lambda e: e.bn_stats(lnst[r, 6:12], src[r, 512:1024]), (key,), ("lnst",))
        S.add("dve", lambda e: e.bn_aggr(lnmv[r, :], lnst[r, :]), ("lnst",), ("lnmv",))
        act(lnr[r, 0:1], lnmv[r, 1:2], AF.Sqrt, ("lnmv", "epsc"), ("lnr",), bias=epsc[r, :])
        S.add("dve", lambda e: e.reciprocal(lnr[r, 0:1], lnr[r, 0:1]), ("lnr",), ("lnr",))
        stt(lnr[r, 1:2], lnmv[r, 0:1], -1.0, lnr[r, 0:1], ALU.mult, ALU.mult, ("lnmv", "lnr"), ("lnr",))
        act(src[r, :], src[r, :], AF.Identity, (key, "lnr"), (key,), bias=lnr[r, 1:2], scale=lnr[r, 0:1])
        tt("pool", src[r, :], src[r, :], gt[r, :], ALU.mult, (key, gk), (key,))
        tt("pool", src[r, :], src[r, :], bt[r, :], ALU.add, (key, bk), (key,))

    def to_featmajor(src, rows, src_key, col0, dst, dst_key):
        for half in range(2):
            pb = PB[4 + half]
            for kk in range(4):
                k = 4 * half + kk
                tr(pb[:, 128 * kk:128 * kk + rows], src[0:rows, 128 * k:128 * k + 128], ident[0:rows, 0:rows],
                   (src_key, CK), (PBK[4 + half],))
            eng = "act" if half == 0 else "dve"
            cp(eng, dst[:, 4 * half:4 * half + 4, col0:col0 + rows],
               pb[:].rearrange("p (k x) -> p k x", k=4)[:, :, 0:rows], (PBK[4 + half],), (dst_key,))

    mainrot = dict(i=0)

    def next_bank():
        i = mainrot["i"] % 4
        mainrot["i"] += 1
        return PB[i], PBK[i]

    NT = PCOLS // 128

    def inproj_fm(name, npieces, evac):
        for pi_ in range(npieces):
            pv, pkey = load_piece(f"{name}{pi_}")
            for j in range(NT):
                pb, pk_ = next_bank()
                for k in range(8):
                    mm(pb[:, 0:T], pv[:, k, 128 * j:128 * j + 128], hT[:, k, :], k == 0, k == 7, (pkey, "hT"), (pk_,))
                evac(NT * pi_ + j, pb[:, 0:T], pk_)

    def chk(n):
        if phase_limit is not None and n >= phase_limit:
            raise _Stop()

    try:
      for s in range(NSTEP if nsteps_limit != 0 else 0):
        for q in range(2):
              first = (s == 0)
              g0t, g0k = load_gb(ln0_g)
              b0t, b0k = load_gb(ln0_b)
              for ti, (c0, rows) in enumerate(TOK_TILES):
                  hk = f"h{ti}"
                  hv_ = h_tok[:, ti, :]
                  if first and ti == 0:
                      mset("pool", hv_[0:NPAD, :], 0.0, (hk,))
                      dma("sp", hv_[NPAD:64, :], meta, (), (hk,), f"xl{ti}")
                      dma("sp", hv_[64:128, :], x2[q, 0:64, :], (), (hk,), f"xl{ti}")
                  else:
                      xs = s * T + c0 - 64
                      dma("sp", hv_[0:rows, :], x2[q, xs:xs + rows, :], (), (hk,), f"xl{ti}")
                  layer_norm(hv_, rows, hk, g0t, g0k, b0t, b0k)
                  to_featmajor(hv_, rows, hk, c0, hT, "hT")

              chk(1)
              def ev_u(j, p_, pk_):
                  act(uT[:, j, :], p_, AF.Identity, (pk_, CK), ("uT",), bias=bias_fm[:, j:j + 1])
              inproj_fm("in_u", 512 // PCOLS, ev_u)
              if first:
                  mset("pool", uT[:, :, 0:NPAD], 0.0, ("uT",))
              chk(1.2)
              for c in range(4):
                  for ri in range(2):
                      for i in range(TC):
                          for ql in range(4):
                              o_ = PB[4 + ql][:, (2 * c + ri) * NCH:(2 * c + ri + 1) * NCH]
                              mm(o_, s5W[32 * ql:32 * ql + 32, c, TC - 1 - i, ri, :],
                                 uT[32 * ql:32 * ql + 32, c, i:T:TC], i == 0, i == TC - 1, ("s5c", "uT"), (PBK[4 + ql],),
                                 tile_position=(32 * ql, 0))
              for ql in range(4):
                  cp("act", V5[:, :, ql:16:4, :], PB[4 + ql][:, 0:8 * NCH].rearrange("p (c r n) -> p r c n", c=4, r=2),
                     (PBK[4 + ql],), ("V5",))
              chk(1.4)
              Er = Epos[:, 0, :, :]; Ei = Epos[:, 1, :, :]
              tt("pool", s5a[:], Er, V5[:, 0, :, :], ALU.mult, ("s5c", "V5"), ("s5a",))
              tt("pool", s5b[:], Ei, V5[:, 1, :, :], ALU.mult, ("s5c", "V5"), ("s5b",))
              tt("pool", Vt[:, 0, :, :], s5a[:], s5b[:], ALU.add, ("s5a", "s5b"), ("Vt",))
              tt("pool", s5a[:], Er, V5[:, 1, :, :], ALU.mult, ("s5c", "V5"), ("s5a",))
              tt("pool", s5b[:], Ei, V5[:, 0, :, :], ALU.mult, ("s5c", "V5"), ("s5b",))
              tt("pool", Vt[:, 1, :, :], s5a[:], s5b[:], ALU.subtract, ("s5a", "s5b"), ("Vt",))
              Yl = Ylast[q]; ylk = f"Ylast{q}"
              tt("pool", s5i[:, 0, :], Gc[:, 0, :], Yl[:, 0, :], ALU.mult, ("s5c", ylk), ("s5i",))
              tt("pool", s5i[:, 1, :], Gc[:, 1, :], Yl[:, 1, :], ALU.mult, ("s5c", ylk), ("s5i",))
              tt("pool", s5i[:, 0, :], s5i[:, 0, :], s5i[:, 1, :], ALU.subtract, ("s5i",), ("s5i",))
              tt("pool", s5i[:, 2, :], Gc[:, 0, :], Yl[:, 1, :], ALU.mult, ("s5c", ylk), ("s5i",))
              tt("pool", s5i[:, 3, :], Gc[:, 1, :], Yl[:, 0, :], ALU.mult, ("s5c", ylk), ("s5i",))
              tt("pool", s5i[:, 2, :], s5i[:, 2, :], s5i[:, 3, :], ALU.add, ("s5i",), ("s5i",))
              tt("pool", Vt[:, 0, :, 0], Vt[:, 0, :, 0], s5i[:, 0, :], ALU.add, ("Vt", "s5i"), ("Vt",))
              tt("pool", Vt[:, 1, :, 0], Vt[:, 1, :, 0], s5i[:, 2, :], ALU.add, ("Vt", "s5i"), ("Vt",))
              chk(1.6)
              for ri in range(2):
                  S.add("dve", lambda e, ri=ri: e.tensor_tensor_scan(
                      Y5[:, ri, :, :].rearrange("p q n -> p (q n)"), D0[:].rearrange("p q n -> p (q n)"),
                      Vt[:, ri, :, :].rearrange("p q n -> p (q n)"), 0.0, ALU.mult, ALU.add), ("s5c", "Vt", "V5"), ("V5",))
              for ri in range(2):
                  cp("pool", Yl[:, ri, :], Y5[:, ri, :, NCH - 1], ("V5",), (ylk,))
              xk = f"Xh{q}"
              cp("pool", Xh[q][:, :, :, 0:1], Xh[q][:, :, :, NCH:NCH + 1], (xk,), (xk,))
              tt("pool", s5a[:], Er, Y5[:, 0, :, :], ALU.mult, ("s5c", "V5"), ("s5a",))
              tt("pool", s5b[:], Ei, Y5[:, 1, :, :], ALU.mult, ("s5c", "V5"), ("s5b",))
              tt("pool", Xh[q][:, 0, :, 1:NCH + 1], s5a[:], s5b[:], ALU.subtract, ("s5a", "s5b"), (xk,))
              tt("pool", s5a[:], Er, Y5[:, 1, :, :], ALU.mult, ("s5c", "V5"), ("s5a",))
              tt("pool", s5b[:], Ei, Y5[:, 0, :, :], ALU.mult, ("s5c", "V5"), ("s5b",))
              tt("pool", Xh[q][:, 1, :, 1:NCH + 1], s5a[:], s5b[:], ALU.add, ("s5a", "s5b"), (xk,))

              chk(2)
              def ev_gs(j, p_, pk_):
                  act(sgs[:, j, :], p_, AF.Sigmoid, (pk_, CK), ("sgs",), bias=bias_fm[:, 28 + j:29 + j])
              inproj_fm("in_gs", 1024 // PCOLS, ev_gs)

              chk(3)
              for qk_i, nm in enumerate(("in_q", "in_k")):
                  def ev_qk(j, p_, pk_, qk_i=qk_i):
                      tile = 4 * qk_i + j
                      ri_ = tile % 2
                      rw = raw[ri_]; rk = f"raw{ri_}"
                      ca = cacc[ri_]; ck_ = f"cacc{ri_}"
                      act(rw[:, 3:T + 3], p_, AF.Identity, (pk_, CK), (rk,), bias=bias_fm[:, 4 + tile:5 + tile])
                      if first:
                          mset("pool", rw[:, 3:3 + NPAD], 0.0, (rk,))
                      cp("pool", rw[:, 0:3], halo[q][:, tile, :], (f"halo{q}",), (rk,))
                      cp("pool", halo[q][:, tile, :], rw[:, T:T + 3], (rk,), (f"halo{q}",))
                      ts("dve", ca[:], rw[:, 0:T], cw[:, tile, 0:1], cb[:, tile:tile + 1], ALU.mult, ALU.add, (rk, CK), (ck_,))
                      for w_ in range(1, 4):
                          stt(ca[:], rw[:, w_:w_ + T], cw[:, tile, w_:w_ + 1], ca[:], ALU.mult, ALU.add, (rk, CK, ck_), (ck_,))
                      act(qkT[:, tile, :], ca[:], AF.Silu, (ck_,), ("qkT",))
                  inproj_fm(nm, 512 // PCOLS, ev_qk)

              chk(4)
              for c in range(4):
                  pb = PB[5]
                  Py = pb[:, 0:TC * NCH].rearrange("p (j n) -> p j n", j=TC)
                  for j in range(TC):
                      for i in range(j + 1):
                          mm(Py[:, j, :], s5K[:, c, j - i, :], uT[:, c, i:T:TC], i == 0, False, ("s5c", "uT"), (PBK[5],))
                      for ql in range(4):
                          for ri in range(2):
                              last = (ql == 3 and ri == 1)
                              mm(Py[32 * ql:32 * ql + 32, j, :], s5C[:, j, ri, 4 * c + ql, :], Xh[q][:, ri, 4 * c + ql, 0:NCH],
                                 False, last, ("s5c", xk), (PBK[5],), tile_position=(0, 32 * ql))
                  cp("act", yS[:].rearrange("p (n j) -> p j n", j=TC), Py, (PBK[5],), ("yS",))
                  tt("pool", g1[:], yS[:], yS[:], ALU.mult, ("yS",), ("g1",))
                  ts("pool", g1[:], g1[:], 0.044715, 1.0, ALU.mult, ALU.add, ("g1",), ("g1",))
                  tt("pool", g1[:], g1[:], yS[:], ALU.mult, ("g1", "yS"), ("g1",))
                  act(g2[:], g1[:], AF.Sigmoid, ("g1",), ("g2",), scale=1.5957691216057308)
                  tt("pool", gT[:, c, :], g2[:], yS[:], ALU.mult, ("g2", "yS"), ("gT",))
              for hh in range(2):
                  pv2, pk2 = load_piece(f"glu2_{hh}")
                  pv1, pk1 = load_piece(f"glu1_{hh}")
                  for j in range(4):
                      jj = 4 * hh + j
                      sz = sgzt[jj % 2]; szk = f"sgzt{jj % 2}"
                      pb, pk_ = next_bank()
                      for k in range(4):
                          mm(pb[:, 0:T], pv2[:, k, 128 * j:128 * j + 128], gT[:, k, :], k == 0, k == 3, (pk2, "gT"), (pk_,))
                      act(sz[:], pb[:, 0:T], AF.Sigmoid, (pk_,), (szk,))
                      pb, pk_ = next_bank()
                      for k in range(4):
                          mm(pb[:, 0:T], pv1[:, k, 128 * j:128 * j + 128], gT[:, k, :], k == 0, k == 3, (pk1, "gT"), (pk_,))
                      tt("dve", gtmp[:], pb[:, 0:T], sz[:], ALU.mult, (pk_, szk), ("gtmp",))
                      tt("pool", sgs[:, jj, :], gtmp[:], sgs[:, jj, :], ALU.mult, ("gtmp", "sgs"), ("sgs",))

              chk(5)
              for gi in range(2):
                  for k in range(8):
                      mm(PB[7 - gi][0:4, 0:T], wg[:, k, 4 * gi:4 * gi + 4], hT[:, k, :], k == 0, k == 7,
                         ("wg", "hT"), (PBK[7 - gi],))
              act(gli[:], PB[7][0:4, 0:T], AF.Identity, (PBK[7], CK), ("gli",), bias=bias_g[:, 0:1])
              ts("dve", glf[:], PB[6][0:4, 0:T], bias_g[:, 1:2], -1.0, ALU.add, ALU.mult, (PBK[6], CK), ("glf",))
              act(glf[:], glf[:], AF.Exp, ("glf",), ("glf",))
              act(glf[:], glf[:], AF.Ln, ("glf", "onec"), ("glf",), bias=onec[0:4, :])
              ts("dve", glf[:], glf[:], -1.0, None, ALU.mult, None, ("glf",), ("glf",))
              if first:
                  mset("dve", glf[:, 0:NPAD], 0.0, ("glf",))
                  mset("dve", gli[:, 0:NPAD], -10000.0, ("gli",))
              S.add("dve", lambda e: e.tensor_tensor_scan(gb_[:], seg[:], glf[:], 0.0, ALU.mult, ALU.add),
                    ("glf", CK), ("gbcum",))
              tt("dve", gz[:], gli[:], gb_[:], ALU.subtract, ("gli", "gbcum"), ("gz",))
              S.add("dve", lambda e: e.tensor_reduce(gM[:], gz[:].rearrange("p (c j) -> p c j", j=MCH), AX.X, ALU.max),
                    ("gz",), ("gM",))
              cp("dve", gbl[:], gb_[:, MCH - 1:T:MCH], ("gbcum",), ("gbl",))
              mk = f"mcar{q}"
              S.add("dve", lambda e, q=q: e.tensor_tensor_scan(gma[:], gM[:], gbl[:], mcar[q][:, 0:1], ALU.max, ALU.add),
                    ("gM", "gbl", mk), ("gma",))
              cp("dve", gms[:, 0:1], mcar[q][:, 0:1], (mk,), ("gms",))
              cp("dve", gms[:, 1:NMC], gma[:, 0:NMC - 1], ("gma", "gms"), ("gms",))
              cp("dve", mcar[q][:, 0:1], gma[:, NMC - 1:NMC], ("gma", "gms"), (mk,))
              tt("dve", gmu[:], gms[:], gM[:], ALU.max, ("gms", "gM"), ("gmu",))
              tt("dve", gdl[:], gms[:], gmu[:], ALU.subtract, ("gms", "gmu"), ("gdl",))
              act(gdl[:], gdl[:], AF.Exp, ("gdl",), ("gdl",))
              mub = gmu[:].unsqueeze(2).broadcast_to([4, NMC, MCH])
              tt("dve", gw[:].rearrange("p (c j) -> p c j", j=MCH), gz[:].rearrange("p (c j) -> p c j", j=MCH), mub,
                 ALU.subtract, ("gz", "gmu"), ("gw",))
              act(gw[:], gw[:], AF.Exp, ("gw", "lnkc"), ("gw",), bias=lnkc[0:4, :])
              tt("dve", grho[:].rearrange("p (c j) -> p c j", j=MCH), gb_[:].rearrange("p (c j) -> p c j", j=MCH), mub,
                 ALU.add, ("gbcum", "gmu"), ("grho",))
              act(grho[:], grho[:], AF.Exp, ("grho",), ("grho",), scale=-1.0)
              ogv = PB[7][0:64, 0:NMC * 8].rearrange("p (c x h) -> p c x h", c=NMC, x=2)
              for c in range(NMC):
                  tr(ogv[:, c, 0, :], gw[:, MCH * c:MCH * c + MCH], ident[0:4, 0:4], ("gw", CK), (PBK[7],))
                  tr(ogv[:, c, 1, :], grho[:, MCH * c:MCH * c + MCH], ident[0:4, 0:4], ("grho", CK), (PBK[7],))
              cp("dve", og[:], ogv, (PBK[7],), ("og",))
              pbd = PB[6]
              for h in range(NH):
                  mm(pbd[:, NMC * h:NMC * h + NMC], sel4[:, 128 * h:128 * h + 128], gdl[:], True, True, (CK, "gdl"), (PBK[6],))
              cp("dve", dlb[:], pbd[:, 0:NH * NMC].rearrange("p (h c) -> p h c", h=NH), (PBK[6],), ("dlb",))

              chk(6)
              for pi_ in range(1024 // PCOLS):
                  pv, pkey = load_piece(f"in_v{pi_}")
                  for c in range(NMC):
                      pb, pk_ = next_bank()
                      for k in range(8):
                          mm(pb[0:64, 0:PCOLS], hT[:, k, MCH * c:MCH * c + MCH], pv[:, k, :], k == 0, False, (pkey, "hT"), (pk_,))
                      mm(pb[0:64, 0:PCOLS], ones1[:, :], bvrow[:, PCOLS * pi_:PCOLS * pi_ + PCOLS], False, True,
                         ("ones1", "bvrow"), (pk_,))
                      cp("act" if c % 2 else "dve", v_tok[:, c, PCOLS * pi_:PCOLS * pi_ + PCOLS], pb[0:64, 0:PCOLS], (pk_,), ("v_tok",))

              def ev_o(j, p_, pk_):
                  act(sgo[:, j, :], p_, AF.Sigmoid, (pk_, CK), ("sgo",), bias=bias_fm[:, 20 + j:21 + j])
              inproj_fm("in_o", 1024 // PCOLS, ev_o)

              chk(7)
              Sq = St[q]; nq = nS[q]; Sk = f"St{q}"; nk_ = f"nS{q}"
              for c in range(NMC):
                  tk = slice(MCH * c, MCH * c + MCH)
                  tt("dve", Sq[:], Sq[:], dlb[:, :, c:c + 1].broadcast_to([128, NH, DV]), ALU.mult, (Sk, "dlb"), (Sk,))
                  tt("pool", nq[:], nq[:], dlb[:, :, c], ALU.mult, (nk_, "dlb"), (nk_,))
                  cp("act", Sbf[:], Sq[:], (Sk,), ("Sbf",))
                  cp("pool", nbf[:], nq[:], (nk_,), ("nbf",))
                  for h in range(NH):
                      mm(PB[5][0:64, 64 * h:64 * h + 64], qkT[:, 4 + h, tk], qkT[:, h, tk], True, True, ("qkT",), (PBK[5],))
                  wv = og[:, c, 0, :].unsqueeze(2).broadcast_to([64, NH, 64])
                  tt("dve", sTa[:], PB[5][0:64, 0:256].rearrange("p (h s) -> p h s", h=NH), wv, ALU.mult, (PBK[5], "og"), ("sTa",))
                  tt("pool", sTm[:], sTa[:], caus[:].unsqueeze(1).broadcast_to([64, NH, 64]), ALU.mult, ("sTa", CK), ("sTm",))
                  for h in range(NH):
                      tr(PBb6[0:64, 128 * h:128 * h + 128], qkT[:, 4 + h, tk], identb[:], ("qkT", "identb"), (PBK[6],))
                  wk_ = og[:, c, 0, :].unsqueeze(2).broadcast_to([64, NH, 128])
                  tt("dve", ktl[:], PBb6[0:64, 0:512].rearrange("p (h d) -> p h d", h=NH), wk_, ALU.mult, (PBK[6], "og"), ("ktl",))
                  for h in range(NH):
                      pb = PB[h // 2]; pk_ = PBK[h // 2]
                      o_ = pb[0:64, 256 * (h % 2):256 * (h % 2) + 256]
                      mm(o_, sTm[:, h, :], v_tok[:, c, 256 * h:256 * h + 256], True, False, ("sTm", "v_tok"), (pk_,))
                      mm(o_, qkT[:, h, tk], Sbf[:, h, :], False, True, ("qkT", "Sbf"), (pk_,))
                  for h in range(NH):
                      mm(PB[7][0:64, h:h + 1], sTm[:, h, :], ones64[:, 0:1], True, False, ("sTm", "ones64"), (PBK[7],))
                      mm(PB[7][0:64, h:h + 1], qkT[:, h, tk], nbf[:, h:h + 1], False, True, ("qkT", "nbf"), (PBK[7],))
                  for h in range(NH):
                      pb = PB[2 + h // 2]; pk_ = PBK[2 + h // 2]
                      mm(pb[:, 256 * (h % 2):256 * (h % 2) + 256], ktl[:, h, :], v_tok[:, c, 256 * h:256 * h + 256], True, True,
                         ("ktl", "v_tok"), (pk_,))
                  for h in range(NH):
                      mm(PB[7][:, 8 + h:9 + h], ktl[:, h, :], ones64[:, 0:1], True, True, ("ktl", "ones64"), (PBK[7],))
                  act(hden[:], PB[7][0:64, 0:4], AF.Abs, (PBK[7],), ("hden",))
                  tt("dve", hden[:], hden[:], og[:, c, 1, :], ALU.max, ("hden", "og"), ("hden",))
                  S.add("dve", lambda e: e.reciprocal(hr[:], hden[:]), ("hden",), ("hr",))
                  for h in range(NH):
                      pb = PB[h // 2]; pk_ = PBK[h // 2]
                      S.add("dve", lambda e, h=h, pb=pb: e.bn_stats(hst[:, h, :], pb[0:64, 256 * (h % 2):256 * (h % 2) + 256]),
                            (pk_,), ("hst",))
                  for h in range(NH):
                      S.add("dve", lambda e, h=h: e.bn_aggr(hmv[:, h, :], hst[:, h, :]), ("hst",), ("hmv",))
                  tt("dve", hv1[:], hr[:], hr[:], ALU.mult, ("hr",), ("hv1",))
                  tt("dve", hv1[:], hv1[:], hmv[:, :, 1], ALU.mult, ("hv1", "hmv"), ("hv1",))
                  act(hv1[:], hv1[:], AF.Sqrt, ("hv1", "epsc"), ("hv1",), bias=epsc[0:64, :])
                  S.add("dve", lambda e: e.reciprocal(hv1[:], hv1[:]), ("hv1",), ("hv1",))
                  tt("dve", hsc[:], hv1[:], hr[:], ALU.mult, ("hv1", "hr"), ("hsc",))
                  stt(hbi[:], hmv[:, :, 0], -1.0, hsc[:], ALU.mult, ALU.mult, ("hmv", "hsc"), ("hbi",))
                  for h in range(NH):
                      pb = PB[h // 2]; pk_ = PBK[h // 2]
                      act(hn[:, 256 * h:256 * h + 256], pb[0:64, 256 * (h % 2):256 * (h % 2) + 256], AF.Identity,
                          (pk_, "hsc", "hbi"), ("hn",), bias=hbi[:, h:h + 1], scale=hsc[:, h:h + 1])
                  for hh in range(2):
                      tt("dve", Sq[:, 2 * hh:2 * hh + 2, :], Sq[:, 2 * hh:2 * hh + 2, :],
                         PB[2 + hh][:].rearrange("p (h d) -> p h d", h=2), ALU.add, (Sk, PBK[2 + hh]), (Sk,))
                  tt("dve", nq[:], nq[:], PB[7][:, 8:12], ALU.add, (nk_, PBK[7]), (nk_,))
                  for k in range(8):
                      tr(PBb4[:, 64 * k:64 * k + 64], hn[:, 128 * k:128 * k + 128], identb[0:64, 0:64], ("hn", "identb"), (PBK[4],))
                  for k in range(8):
                      stt(gatedT[:, k, tk], PBb4[:, 64 * k:64 * k + 64], mng[:, k:k + 1], sgo[:, k, tk], ALU.mult, ALU.mult,
                          (PBK[4], CK, "sgo"), ("gatedT",))

              chk(8)
              def ev_gm(j, p_, pk_):
                  act(sgo[:, j, :], p_, AF.Sigmoid, (pk_, CK), ("sgo",), bias=bias_fm[:, 36 + j:37 + j])
              inproj_fm("in_gm", 1024 // PCOLS, ev_gm)
              for pi_ in range(1024 // PCOLS):
                  pv, pkey = load_piece(f"mwo{pi_}")
                  for j in range(NT):
                      jj = NT * pi_ + j
                      pb, pk_ = next_bank()
                      for k in range(8):
                          mm(pb[:, 0:T], pv[:, k, 128 * j:128 * j + 128], gatedT[:, k, :], k == 0, k == 7, (pkey, "gatedT"), (pk_,))
                      tt("dve", gtmp[:], pb[:, 0:T], sgo[:, jj, :], ALU.mult, (pk_, "sgo"), ("gtmp",))
                      tt("pool", mixT[:, jj, :], gtmp[:], sgs[:, jj, :], ALU.add, ("gtmp", "sgs"), ("qkT",))

              chk(9)
              wop = [load_piece(f"wo{i}") for i in range(4)]
              g1t, g1k = load_gb(ln1_g)
              b1t, b1k = load_gb(ln1_b)
              for ti, (c0, rows) in enumerate(TOK_TILES):
                  hk = f"h{ti}"
                  for pi_ in range(4):
                      pb, pk_ = next_bank()
                      pv, pkey = wop[pi_]
                      for k in range(8):
                          mm(pb[0:rows, 0:PCOLS], mixT[:, k, c0:c0 + rows], pv[:, k, :], k == 0, k == 7, (pkey, "qkT"), (pk_,))
                      hv = h_tok[0:rows, ti, PCOLS * pi_:PCOLS * pi_ + PCOLS]
                      stt(hv, hv, ALPHA, pb[0:rows, 0:PCOLS], ALU.mult, ALU.add, (hk, pk_), (hk,))
                  layer_norm(h_tok[:, ti, :], rows, hk, g1t, g1k, b1t, b1k)
                  to_featmajor(h_tok[:, ti, :], rows, hk, c0, hT, "hT")

              chk(10)
              for qf in range(4):
                  at = actT[qf % 2]; ak = actK[qf % 2]
                  for pi_ in range(4):
                      pv, pkey = load_piece(f"up{qf}_{pi_}")
                      for j in range(NT):
                          f_ = NT * pi_ + j
                          pb, pk_ = next_bank()
                          for k in range(8):
                              mm(pb[:, 0:T], pv[:, k, 128 * j:128 * j + 128], hT[:, k, :], k == 0, k == 7, (pkey, "hT"), (pk_,))
                          r_ = rl[f_ % 2]; rk_ = f"rl{f_ % 2}"
                          act(r_[:], pb[:, 0:T], AF.Relu, (pk_, CK), (rk_,), bias=bup[:, 8 * qf + f_:8 * qf + f_ + 1])
                          tt("pool", at[:, f_, :], r_[:], r_[:], ALU.mult, (rk_,), (ak,))
                  dnp = [load_piece(f"dn{qf}_{i}") for i in range(4)]
                  for ti, (c0, rows) in enumerate(TOK_TILES):
                      hk = f"h{ti}"
                      for pi_ in range(4):
                          pb, pk_ = next_bank()
                          pv, pkey = dnp[pi_]
                          for k in range(8):
                              mm(pb[0:rows, 0:PCOLS], at[:, k, c0:c0 + rows], pv[:, k, :], k == 0, k == 7, (pkey, ak), (pk_,))
                          hv = h_tok[0:rows, ti, PCOLS * pi_:PCOLS * pi_ + PCOLS]
                          if qf == 0:
                              stt(hv, hv, ALPHA, pb[0:rows, 0:PCOLS], ALU.mult, ALU.add, (hk, pk_), (hk,))
                          else:
                              tt("dve", hv, hv, pb[0:rows, 0:PCOLS], ALU.add, (hk, pk_), (hk,))

              chk(11)
              g2t, g2k = load_gb(ln2_g)
              b2t, b2k = load_gb(ln2_b)
              for ti, (c0, rows) in enumerate(TOK_TILES):
                  hk = f"h{ti}"
                  layer_norm(h_tok[:, ti, :], rows, hk, g2t, g2k, b2t, b2k)
                  if first and ti == 0:
                      dma("sp", y_out[q, 0:64, :], h_tok[64:128, ti, :], (hk,), ("yout",), f"ost{ti}")
                  else:
                      xs = s * T + c0 - 64
                      dma("sp", y_out[q, xs:xs + rows, :], h_tok[0:rows, ti, :], (hk,), ("yout",), f"ost{ti}")
              if dbg and (s, q) == dbg_at:
                  for name, (ap_, keys) in dbg_taps.items():
                      dma("sp", dbg_out[name], ap_, keys, ("dbgout",), "dbg")
              wstate["first"] = False
              if nsteps_limit is not None and (s * 2 + q + 1) >= nsteps_limit:
                  raise _Stop()
    except _Stop:
        pass

    counts = S.finalize()
    S.check_no_deadlock()
    sems = {}
    for key in counts:
        sems[key] = es.enter_context(nc.semaphore(f"{key[0]}_{key[1]}"))
    for key in [("eng", e_) for e_ in S.streams]:
        if key not in sems:
            sems[key] = es.enter_context(nc.semaphore(f"{key[0]}_{key[1]}"))
    final_waits = [(k, v) for k, v in counts.items() if k[0] == "dma" and (k[1].startswith("ost") or k[1] == "dbg")]
    block = es.enter_context(nc.Block())

    @block.sync
    def _(e):
        S.emit("sp", e, sems)
        for k, v in final_waits:
            e.wait_ge(sems[k], v)

    @block.tensor
    def _(e):
        S.emit("pe", e, sems)

    @block.scalar
    def _(e):
        S.emit("act", e, sems)

    @block.vector
    def _(e):
        S.emit("dve", e, sems)

    @block.gpsimd
    def _(e):
        S.emit("pool", e, sems)

    es.close()
    print("ops per stream:", {k: len(v) for k, v in S.streams.items()}, "max sem:", max(counts.values()))
    return nc


def _consts():
    ident = np.eye(128, dtype=np.float32)
    idx = np.arange(128)
    bd16 = (idx[:, None] // 16 == idx[None, :] // 16).astype(np.float32)
    j = np.arange(64)
    caus = (j[:, None] <= j[None, :]).astype(np.float32)
    par = np.zeros((128, 4), np.float32)
    for t in range(2):
        par[:, t] = (idx // 64 == t)
        par[:, 2 + t] = ((idx // 16) % 2 == t)
    seg = np.ones((4, T), np.float32)
    seg[:, 0::MCH] = 0.0
    sel4 = np.zeros((4, 4, 128), np.float32)
    for h in range(4):
        sel4[h, h, :] = 1.0
    return dict(c_ident=ident, c_bd16=bd16, c_caus=caus, c_par=par, c_seg=seg, c_sel4=sel4.reshape(4, 512))


_NC_CACHE = {}


def kernel(**inputs):
    n_cores = 8
    if "nc" not in _NC_CACHE:
        _NC_CACHE["nc"] = build_program()
    nc = _NC_CACHE["nc"]
    x = np.ascontiguousarray(inputs["x"], dtype=np.float32)
    consts = _consts()
    shared = {}
    for k, v in inputs.items():
        if k == "x":
            continue
        a = np.ascontiguousarray(v, dtype=np.float32)
        if a.ndim >= 2 and a.shape[0] == 1 and k not in ("meta_tokens",):
            a = a[0]
        shared[k] = np.ascontiguousarray(a)
    shared.update(consts)
    in_maps = []
    for c in range(n_cores):
        m = dict(shared)
        m["x2"] = np.ascontiguousarray(x[2 * c:2 * c + 2])
        in_maps.append(m)
    res = run_bass_kernel_spmd(nc, in_maps, core_ids=list(range(n_cores)))
    out = np.concatenate([r["y"] for r in res.results], axis=0)
    return out.astype(np.float32, copy=False)
```
